# Optimizing a Trainium2 kernel written in Bass

```python
import jax, jax.numpy as jnp
from jax import lax
import numpy as np

D_MODEL = 4096
BATCH = 2
SEQ = 4096
DEPTH = 2

HEAD_DIM = 128
D_MIX = D_MODEL
LRU_WIDTH = D_MIX // 4
LRU_BLOCKS = LRU_WIDTH // HEAD_DIM
LRU_CONV = 4
LRU_C = 8.0
HGRN_WIDTH = D_MIX // 4
HGRN_HEADS = HGRN_WIDTH // HEAD_DIM
HGRN_CHUNK = 64
FOX_WIDTH = D_MIX // 2
FOX_HEADS = FOX_WIDTH // HEAD_DIM
FOX_BLOCK = 128
D_IN = 2 * LRU_WIDTH + 4 * HGRN_WIDTH + 3 * FOX_WIDTH + FOX_HEADS
D_FF = ((8 * D_MODEL // 3 + 255) // 256) * 256
FFN_CONV = 3
EPS = 1e-6

kernel_name = "hybrid_rglru_hgrn2_fox_parallel_heads"

F32 = jnp.float32


def _split_points():
    sizes = (LRU_WIDTH, LRU_WIDTH,
             HGRN_WIDTH, HGRN_WIDTH, HGRN_WIDTH, HGRN_WIDTH,
             FOX_WIDTH, FOX_WIDTH, FOX_WIDTH, FOX_HEADS)
    return [int(v) for v in np.cumsum(sizes)[:-1]]


def rms_norm(x, gain):
    xf = x.astype(F32)
    var = jnp.mean(xf * xf, axis=-1, keepdims=True)
    return (xf * lax.rsqrt(var + EPS) * gain.astype(F32)).astype(x.dtype)


def head_rms_norm(t, gain):
    b, s, w = t.shape
    th = t.astype(F32).reshape(b, s, w // HEAD_DIM, HEAD_DIM)
    var = jnp.mean(th * th, axis=-1, keepdims=True)
    return (th * lax.rsqrt(var + EPS)).reshape(b, s, w) * gain.astype(F32)


def causal_dwconv(x, w, bias):
    width, ch = w.shape
    y = lax.conv_general_dilated(
        x, w[:, None, :].astype(x.dtype), window_strides=(1,), padding=[(width - 1, 0)],
        dimension_numbers=("NWC", "WIO", "NWC"), feature_group_count=ch)
    return y + bias.astype(x.dtype)


def rg_lru(xc, w_a, b_a, w_x, b_x, lam):
    b, s, _ = xc.shape
    xh = xc.reshape(b, s, LRU_BLOCKS, HEAD_DIM)
    r = jax.nn.sigmoid(jnp.einsum("bsnd,nde->bsne", xh, w_a).reshape(b, s, LRU_WIDTH).astype(F32) + b_a.astype(F32))
    i = jax.nn.sigmoid(jnp.einsum("bsnd,nde->bsne", xh, w_x).reshape(b, s, LRU_WIDTH).astype(F32) + b_x.astype(F32))
    log_a = -LRU_C * r * jax.nn.softplus(-lam.astype(F32))
    a = jnp.exp(log_a)
    u = jnp.sqrt(-jnp.expm1(2.0 * log_a)) * (i * xc.astype(F32))

    def combine(left, right):
        a1, h1 = left
        a2, h2 = right
        return a1 * a2, a2 * h1 + h2

    _, h = lax.associative_scan(combine, (a, u), axis=1)
    return h


def hgrn2_chunkwise(q, log_f, k, v):
    b, s, h, dk = q.shape
    dv = v.shape[-1]
    n = s // HGRN_CHUNK

    def to_chunks(t):
        return t.reshape(b, n, HGRN_CHUNK, h, t.shape[-1]).transpose(1, 0, 3, 2, 4)

    causal = jnp.tril(jnp.ones((HGRN_CHUNK, HGRN_CHUNK), dtype=bool))

    def step(state, inp):
        qc, gc, kc, vc = inp
        cum = jnp.cumsum(gc, axis=2)
        o_inter = jnp.einsum("bhtk,bhkv->bhtv", qc * jnp.exp(cum), state)
        diff = cum[:, :, :, None, :] - cum[:, :, None, :, :]
        decay = jnp.exp(jnp.where(causal[:, :, None], diff, -jnp.inf))
        scores = jnp.einsum("bhtk,bhsk,bhtsk->bhts", qc, kc, decay)
        o_intra = jnp.einsum("bhts,bhsv->bhtv", scores, vc)
        last = cum[:, :, -1:, :]
        new_state = (jnp.exp(last[:, :, 0, :])[..., None] * state
                     + jnp.einsum("bhsk,bhsv->bhkv", kc * jnp.exp(last - cum), vc))
        return new_state, o_inter + o_intra

    state0 = jnp.zeros((b, h, dk, dv), F32)
    _, o = lax.scan(step, state0, (to_chunks(q), to_chunks(log_f), to_chunks(k), to_chunks(v)))
    return o.transpose(1, 0, 3, 2, 4).reshape(b, s, h * dv)


def forgetting_attention(q, k, v, log_f):
    b, s, h, d = q.shape
    cum = jnp.cumsum(log_f, axis=1).transpose(0, 2, 1)
    scale = d ** -0.5
    outs = []
    for blk in range(s // FOX_BLOCK):
        q0, q1 = blk * FOX_BLOCK, (blk + 1) * FOX_BLOCK
        logits = jnp.einsum("bqhd,bkhd->bhqk", q[:, q0:q1], k[:, :q1]).astype(F32) * scale
        logits = logits + cum[:, :, q0:q1, None] - cum[:, :, None, :q1]
        mask = (q0 + jnp.arange(FOX_BLOCK))[:, None] >= jnp.arange(q1)[None, :]
        probs = jax.nn.softmax(jnp.where(mask, logits, -jnp.inf), axis=-1)
        outs.append(jnp.einsum("bhqk,bkhd->bqhd", probs.astype(v.dtype), v[:, :q1]))
    return jnp.concatenate(outs, axis=1).reshape(b, s, h * d)


def setup_inputs(seed: int = 0) -> dict:
    key = jax.random.key(seed)
    ks = jax.random.split(key, 24)
    nrm = jax.random.normal
    p_a0 = jax.random.uniform(ks[9], (DEPTH, LRU_WIDTH), F32, 0.9, 0.999) ** (1.0 / LRU_C)
    return {
        "x": nrm(ks[0], (BATCH, SEQ, D_MODEL), F32),
        "ln1_w": 1.0 + 0.02 * nrm(ks[1], (DEPTH, D_MODEL), F32),
        "w_in": nrm(ks[2], (DEPTH, D_MODEL, D_IN), F32) * D_MODEL ** -0.5,
        "lru_conv_w": nrm(ks[3], (DEPTH, LRU_CONV, LRU_WIDTH), F32) * LRU_CONV ** -0.5,
        "lru_conv_b": 0.01 * nrm(ks[4], (DEPTH, LRU_WIDTH), F32),
        "lru_gate_a_w": nrm(ks[5], (DEPTH, LRU_BLOCKS, HEAD_DIM, HEAD_DIM), F32) * HEAD_DIM ** -0.5,
        "lru_gate_a_b": 0.01 * nrm(ks[6], (DEPTH, LRU_WIDTH), F32),
        "lru_gate_x_w": nrm(ks[7], (DEPTH, LRU_BLOCKS, HEAD_DIM, HEAD_DIM), F32) * HEAD_DIM ** -0.5,
        "lru_gate_x_b": 0.01 * nrm(ks[8], (DEPTH, LRU_WIDTH), F32),
        "lru_lambda": jnp.log(p_a0) - jnp.log1p(-p_a0),
        "hgrn_lb_logits": 0.1 * nrm(ks[10], (DEPTH, HGRN_WIDTH), F32),
        "fox_f_bias": jax.random.uniform(ks[11], (DEPTH, FOX_HEADS), F32, 1.0, 6.0),
        "mix_norm_w": 1.0 + 0.02 * nrm(ks[12], (DEPTH, D_MIX), F32),
        "w_out": nrm(ks[13], (DEPTH, D_MIX, D_MODEL), F32) * D_MIX ** -0.5,
        "ln2_w": 1.0 + 0.02 * nrm(ks[14], (DEPTH, D_MODEL), F32),
        "ffn_w_up": nrm(ks[15], (DEPTH, D_MODEL, 2 * D_FF), F32) * D_MODEL ** -0.5,
        "ffn_conv_w": nrm(ks[16], (DEPTH, FFN_CONV, 2 * D_FF), F32) * FFN_CONV ** -0.5,
        "ffn_conv_b": 0.01 * nrm(ks[17], (DEPTH, 2 * D_FF), F32),
        "ffn_w_down": nrm(ks[18], (DEPTH, D_FF, D_MODEL), F32) * D_FF ** -0.5,
        "final_norm_w": 1.0 + 0.02 * nrm(ks[19], (D_MODEL,), F32),
    }


def reference(x, ln1_w, w_in, lru_conv_w, lru_conv_b, lru_gate_a_w, lru_gate_a_b, lru_gate_x_w,
              lru_gate_x_b, lru_lambda, hgrn_lb_logits, fox_f_bias, mix_norm_w, w_out, ln2_w,
              ffn_w_up, ffn_conv_w, ffn_conv_b, ffn_w_down, final_norm_w):
    dt = x.dtype
    b, s, _ = x.shape
    splits = _split_points()
    lb_cum = jnp.cumsum(jax.nn.softmax(hgrn_lb_logits.astype(F32), axis=0), axis=0)
    lower_bounds = lb_cum - lb_cum[0:1]
    g_lru_end, g_hg_end = LRU_WIDTH, LRU_WIDTH + HGRN_WIDTH

    for l in range(DEPTH):
        hn = rms_norm(x, ln1_w[l])
        proj = jnp.einsum("bsd,de->bse", hn, w_in[l])
        (lru_x, lru_y, hg_q, hg_f, hg_i, hg_g,
         fx_q, fx_k, fx_v, fx_f) = jnp.split(proj, splits, axis=-1)
        gains = mix_norm_w[l]

        xc = causal_dwconv(lru_x, lru_conv_w[l], lru_conv_b[l])
        h_lru = rg_lru(xc, lru_gate_a_w[l], lru_gate_a_b[l], lru_gate_x_w[l], lru_gate_x_b[l], lru_lambda[l])
        out_lru = head_rms_norm(h_lru, gains[:g_lru_end]) * jax.nn.gelu(lru_y.astype(F32))

        lb = lower_bounds[l]
        z = hg_f.astype(F32)
        log_f_hg = jnp.logaddexp(jnp.log(lb), jnp.log1p(-lb) + jax.nn.log_sigmoid(z))
        k_hg = (1.0 - lb) * jax.nn.sigmoid(-z)
        q_hg = jax.nn.silu(hg_q.astype(F32))
        shp = (b, s, HGRN_HEADS, HEAD_DIM)
        o_hg = hgrn2_chunkwise(q_hg.reshape(shp), log_f_hg.reshape(shp), k_hg.reshape(shp),
                               hg_i.astype(F32).reshape(shp))
        out_hg = head_rms_norm(o_hg, gains[g_lru_end:g_hg_end]) * jax.nn.sigmoid(hg_g.astype(F32))

        log_f_fox = jax.nn.log_sigmoid(fx_f.astype(F32) + fox_f_bias[l].astype(F32))
        fshp = (b, s, FOX_HEADS, HEAD_DIM)
        o_fox = forgetting_attention(fx_q.reshape(fshp), fx_k.reshape(fshp), fx_v.reshape(fshp), log_f_fox)
        out_fox = head_rms_norm(o_fox, gains[g_hg_end:])

        mixed = jnp.concatenate([out_lru.astype(dt), out_hg.astype(dt), out_fox.astype(dt)], axis=-1)
        x = x + jnp.einsum("bse,ed->bsd", mixed, w_out[l])

        hn2 = rms_norm(x, ln2_w[l])
        up = causal_dwconv(jnp.einsum("bsd,df->bsf", hn2, ffn_w_up[l]), ffn_conv_w[l], ffn_conv_b[l])
        gate, val = jnp.split(up, 2, axis=-1)
        x = x + jnp.einsum("bsf,fd->bsd", jax.nn.silu(gate) * val, ffn_w_down[l])

    return rms_norm(x, final_norm_w)
```

```python
import contextlib
import numpy as np
import ml_dtypes
import concourse.bass as bass
import concourse.mybir as mybir
from concourse.bass_utils import run_bass_kernel_spmd

F32 = mybir.dt.float32
BF16 = mybir.dt.bfloat16
AF = mybir.ActivationFunctionType
ALU = mybir.AluOpType
EPS = 1e-6

D_MODEL = 4096
D_FF = 11008
SEQ = 4096
BATCH = 2
NCORES = 8
TOK = BATCH * SEQ // NCORES

ENGS = ("tensor", "vector", "scalar", "gpsimd", "sync")


class Em:
    def __init__(self, nc, stack):
        self.nc = nc
        self.stack = stack
        self.q = {e: [] for e in ENGS}
        self.sem = {}
        self.val = {}
        self.waited = {e: {} for e in ENGS}
        self.lastw = {}
        self.readers = {}

    def _sem(self, name):
        if name not in self.sem:
            self.sem[name] = self.stack.enter_context(self.nc.semaphore(name))
            self.val[name] = 0
        return name

    def op(self, eng, fns, reads=(), writes=(), dma=None):
        if callable(fns):
            fns = [fns]
        reads = list(reads)
        writes = list(writes)
        if dma is not None:
            writes.append(("__sem", dma))
        waits = {}

        def need(ev):
            if ev is None:
                return
            s, v = ev
            if self.waited[eng].get(s, 0) >= v:
                return
            if waits.get(s, 0) < v:
                waits[s] = v

        for k in reads:
            need(self.lastw.get(k))
        for k in writes:
            need(self.lastw.get(k))
            for ev in self.readers.get(k, ()):
                need(ev)
        for s, v in waits.items():
            self.waited[eng][s] = v
        if dma is not None:
            s = self._sem("d_" + str(dma))
            amt = 16
        else:
            s = self._sem("e_" + eng)
            amt = 1
        self.val[s] += amt
        ev = (s, self.val[s])
        self.q[eng].append((list(waits.items()), fns, s, amt))
        for k in reads:
            self.readers.setdefault(k, []).append(ev)
        for k in writes:
            self.lastw[k] = ev
            self.readers[k] = []
        return ev

    def finish(self, eng="sync"):
        waits = [(s, v) for s, v in self.val.items() if v > 0]
        self.q[eng].append((waits, [], None, 0))

    def replay(self, block):
        def run(e, items):
            for waits, fns, s, amt in items:
                for ws, wv in waits:
                    e.wait_ge(self.sem[ws], wv)
                ins = None
                for f in fns:
                    ins = f(e)
                if s is not None and ins is not None:
                    ins.then_inc(self.sem[s], amt)

        q = self.q

        @block.tensor
        def _(e):
            run(e, q["tensor"])

        @block.vector
        def _(e):
            run(e, q["vector"])

        @block.scalar
        def _(e):
            run(e, q["scalar"])

        @block.gpsimd
        def _(e):
            run(e, q["gpsimd"])

        @block.sync
        def _(e):
            run(e, q["sync"])


class Ctx:
    def __init__(self):
        self.nc = bass.Bass("TRN2", target_bir_lowering=False)
        self.stack = contextlib.ExitStack()
        self.em = Em(self.nc, self.stack)
        self.nps = 0

    def sb(self, name, shape, dt):
        return self.stack.enter_context(self.nc.sbuf_tensor(name, list(shape), dt))

    def ps(self, name, shape, dt=F32):
        return self.stack.enter_context(self.nc.psum_tensor(name, list(shape), dt))

    def din(self, name, shape, dt):
        return self.nc.dram_tensor(name, list(shape), dt, kind="ExternalInput").ap()

    def dout(self, name, shape, dt):
        return self.nc.dram_tensor(name, list(shape), dt, kind="ExternalOutput").ap()

    def dint(self, name, shape, dt):
        return self.nc.dram_tensor(name, list(shape), dt, kind="Internal").ap()

    def done(self):
        self.em.finish("sync")
        with self.nc.Block() as block:
            self.em.replay(block)
        self.stack.close()
        return self.nc


def mm_group(em, out_ap, pairs, reads, writes, f32=False):
    n = len(pairs)
    fns = []
    for i, (l, r) in enumerate(pairs):
        fns.append(lambda e, l=l, r=r, i=i: e.matmul(out_ap, l, r, start=(i == 0), stop=(i == n - 1)))
    return em.op("tensor", fns, reads=reads, writes=writes)


def emit_ssq_finish(cx, ssq_ps_views, rstd, ncols_list, nfeat, keys_ps, key_rstd, tmp, key_tmp):
    em = cx.em
    off = 0
    for v, n, kp in zip(ssq_ps_views, ncols_list, keys_ps):
        em.op("scalar", lambda e, v=v, off=off, n=n: e.activation(
            out=tmp[:, off:off + n], in_=v, func=AF.Sqrt, bias=cx.eps_t[:, 0:1], scale=1.0 / nfeat),
            reads=[kp], writes=[(key_tmp, off)])
        em.op("vector", lambda e, off=off, n=n: e.reciprocal(out=rstd[:, off:off + n], in_=tmp[:, off:off + n]),
              reads=[(key_tmp, off)], writes=[(key_rstd, off)])
        off += n


def build_phase_c(D=D_MODEL, DFF=D_FF, T=TOK, NG=7):
    cx = Ctx()
    nc, em = cx.nc, cx.em
    NK = D // 128
    NJ = DFF // 128
    NT = T + 2
    NTP = T + 4
    TT = (T + 2) // 3
    assert TT * 3 == T + 2 and TT + 2 <= 512
    TH = T // 2

    xT = cx.din("xT", [D, NT], F32)
    mixT = cx.din("mixT", [D, NT], BF16)
    w_out = cx.din("w_out", [D, D], F32)
    w_up = cx.din("w_up", [D, 2 * DFF], F32)
    w_down = cx.din("w_down", [DFF, D], F32)
    pvec = cx.din("pvec", [128, 2 * NK + 8 * NJ], F32)
    xo = cx.dout("xo", [D, T], F32)
    hno = cx.dout("hno", [D, T], F32)
    x1d = cx.dint("x1d", [D, NT], F32)

    pv = cx.sb("pv", [128, 2 * NK + 8 * NJ], F32)
    ones = cx.sb("ones", [128, 128], F32)
    cx.eps_t = cx.sb("eps_t", [128, 1], F32)
    big = cx.sb("big", [128, NK, NTP], BF16)
    rstd = cx.sb("rstd", [128, NTP], F32)
    tmpr = cx.sb("tmpr", [128, NTP], F32)
    rstd2 = cx.sb("rstd2", [128, T], F32)
    njg = [(NJ + NG - 1 - g) // NG for g in range(NG)]
    NJG = max(njg)
    act = cx.sb("act", [128, NJG, NT], BF16)
    NWS = 3
    wsl = [cx.sb(f"wsl{i}", [128, NK, 256], BF16) for i in range(NWS)]
    wdl = [cx.sb(f"wdl{i}", [128, NJG, 256], BF16) for i in range(2)]
    xcs = [cx.sb(f"xc{i}", [128, NT], F32) for i in range(2)]
    sqs = [cx.sb(f"sq{i}", [128, NT], F32) for i in range(2)]
    tg = [cx.sb(f"tg{i}", [128, 3, TT], F32) for i in range(2)]
    tv = [cx.sb(f"tv{i}", [128, 3, TT], F32) for i in range(2)]
    sg = tg
    psb = [cx.ps(f"psb{i}", [128, 512], F32) for i in range(8)]

    LN2 = lambda k: pv[:, k:k + 1]
    NW = lambda k: pv[:, NK + k:NK + k + 1]
    CW = lambda tap, ch: pv[:, 2 * NK + tap * 2 * NJ + ch: 2 * NK + tap * 2 * NJ + ch + 1]
    CB = lambda ch: pv[:, 2 * NK + 6 * NJ + ch: 2 * NK + 6 * NJ + ch + 1]

    em.op("sync", lambda e: e.dma_start(out=pv[:], in_=pvec[:]), writes=["pv"], dma="pv")
    em.op("vector", lambda e: e.memset(ones[:], 1.0), writes=["ones"])
    em.op("vector", lambda e: e.memset(cx.eps_t[:], EPS), writes=["eps"])
    mixv = mixT.rearrange("(k p) t -> p k t", p=128)
    for k in range(NK):
        em.op("sync", lambda e, k=k: e.dma_start(out=big[:, k, 0:NT], in_=mixv[:, k, :]),
              writes=[("big", k)], dma=("big", k % 4))
    w_outv = w_out.rearrange("(k p) c -> p k c", p=128)
    xTv = xT.rearrange("(k p) t -> k p t", p=128)
    x1v = x1d.rearrange("(k p) t -> k p t", p=128)
    xov = xo.rearrange("(k p) t -> k p t", p=128)
    hnov = hno.rearrange("(k p) t -> k p t", p=128)

    ssq_banks = [5, 6, 7]
    for o in range(NK):
        ws = wsl[o % NWS]
        wkey = ("wsl", o % NWS)
        em.op("gpsimd", lambda e, ws=ws, o=o: e.dma_start(out=ws[:, :, 0:128], in_=w_outv[:, :, o * 128:(o + 1) * 128]),
              writes=[wkey], dma=wkey)
        xc = xcs[o % 2]
        xkey = ("xc", o % 2)
        em.op("sync", lambda e, xc=xc, o=o: e.dma_start(out=xc[:], in_=xTv[o]), writes=[xkey], dma=xkey)
        for tt in range(3):
            bank = (o * 3 + tt) % 5
            pairs = [(ws[:, k, 0:128], big[:, k, tt * TT:(tt + 1) * TT]) for k in range(NK)]
            mm_group(em, psb[bank][:, 0:TT], pairs, reads=[wkey] + [("big", k) for k in range(NK)],
                     writes=[("ps", bank)])
            em.op("vector", lambda e, xc=xc, bank=bank, tt=tt: e.tensor_tensor(
                out=xc[:, tt * TT:(tt + 1) * TT], in0=psb[bank][:, 0:TT], in1=xc[:, tt * TT:(tt + 1) * TT], op=ALU.add),
                reads=[("ps", bank)], writes=[xkey])
        sq = sqs[o % 2]
        skey = ("sq", o % 2)
        em.op("scalar", lambda e, sq=sq, xc=xc: e.activation(out=sq[:], in_=xc[:], func=AF.Square),
              reads=[xkey], writes=[skey])
        em.op("sync", lambda e, xc=xc, o=o: e.dma_start(out=x1v[o], in_=xc[:]), reads=[xkey], writes=[("x1d", o)], dma=xkey)
        for tt in range(3):
            b = ssq_banks[tt]
            em.op("tensor", lambda e, b=b, sq=sq, tt=tt, o=o: e.matmul(
                psb[b][:, 0:TT], ones[:], sq[:, tt * TT:(tt + 1) * TT], start=(o == 0), stop=(o == NK - 1)),
                reads=[skey, "ones"], writes=[("ps", b)])
    emit_ssq_finish(cx, [psb[b][:, 0:TT] for b in ssq_banks], rstd, [TT] * 3, D,
                    [("ps", b) for b in ssq_banks], "rstd", tmpr, "tmpr")

    for k in range(NK):
        xc = xcs[k % 2]
        xkey = ("xc", k % 2)
        em.op("sync", lambda e, xc=xc, k=k: e.dma_start(out=xc[:], in_=x1v[k]), reads=[("x1d", k)], writes=[xkey], dma=xkey)
        em.op("vector", lambda e, xc=xc, k=k: e.scalar_tensor_tensor(
            out=big[:, k, 0:NT], in0=xc[:], scalar=LN2(k), in1=rstd[:, 0:NT], op0=ALU.mult, op1=ALU.mult),
            reads=[xkey, "pv"] + [("rstd", i * TT) for i in range(3)], writes=[("big", k)])
        em.op("gpsimd", lambda e, k=k: e.memset(big[:, k, NT:NTP], 0.0), writes=[("bigpad", k)])

    w_upv = w_up.rearrange("(k p) c -> p k c", p=128)
    w_downv = w_down.rearrange("(j p) d -> p j d", p=128)
    bigkeys = [("big", k) for k in range(NK)] + [("bigpad", k) for k in range(NK)]
    gate_banks = [0, 1, 2]
    val_banks = [3, 4, 5]
    down_banks = [6, 7]
    wcount = NK
    j0 = 0
    for g in range(NG):
        nj = njg[g]
        for jj in range(nj):
            j = j0 + jj
            slot = wcount % NWS
            wcount += 1
            ws = wsl[slot]
            wkey = ("wsl", slot)
            em.op("gpsimd", lambda e, ws=ws, j=j: e.dma_start(out=ws[:, :, 0:128], in_=w_upv[:, :, j * 128:(j + 1) * 128]),
                  writes=[wkey], dma=(wkey, "a"))
            em.op("gpsimd", lambda e, ws=ws, j=j: e.dma_start(out=ws[:, :, 128:256], in_=w_upv[:, :, DFF + j * 128:DFF + (j + 1) * 128]),
                  writes=[(wkey, "v")], dma=(wkey, "b"))
            for (banks, c0, wk, ch) in ((gate_banks, 0, wkey, j), (val_banks, 128, (wkey, "v"), NJ + j)):
                fns = []
                for k in range(NK):
                    for tt in range(3):
                        fns.append(lambda e, k=k, tt=tt, banks=banks, c0=c0, ws=ws: e.matmul(
                            psb[banks[tt]][:, 0:TT + 2], ws[:, k, c0:c0 + 128], big[:, k, tt * TT:tt * TT + TT + 2],
                            start=(k == 0), stop=(k == NK - 1)))
                em.op("tensor", fns, reads=[wk] + bigkeys, writes=[("ps", b) for b in banks])
                isg = banks is gate_banks
                tb = (tg if isg else tv)[j % 2]
                tkey = ("tg" if isg else "tv", j % 2)
                for tt in range(3):
                    pb = psb[banks[tt]]
                    pskey = ("ps", banks[tt])
                    em.op("scalar", lambda e, pb=pb, tb=tb, tt=tt, ch=ch: e.activation(
                        out=tb[:, tt, :], in_=pb[:, 2:TT + 2], func=AF.Identity, bias=CB(ch), scale=CW(2, ch)),
                        reads=[pskey, "pv"], writes=[(tkey, tt)])
                    em.op("vector", lambda e, pb=pb, tb=tb, tt=tt, ch=ch: e.scalar_tensor_tensor(
                        out=tb[:, tt, :], in0=pb[:, 1:TT + 1], scalar=CW(1, ch), in1=tb[:, tt, :], op0=ALU.mult, op1=ALU.add),
                        reads=[pskey, "pv"], writes=[(tkey, tt)])
                    em.op("vector", lambda e, pb=pb, tb=tb, tt=tt, ch=ch: e.scalar_tensor_tensor(
                        out=tb[:, tt, :], in0=pb[:, 0:TT], scalar=CW(0, ch), in1=tb[:, tt, :], op0=ALU.mult, op1=ALU.add),
                        reads=[pskey, "pv"], writes=[(tkey, tt)])
                if isg:
                    sgb = sg[j % 2]
                    em.op("scalar", lambda e, sgb=sgb, tb=tb: e.activation(out=sgb[:], in_=tb[:], func=AF.Silu),
                          reads=[(tkey, tt) for tt in range(3)], writes=[(tkey, tt) for tt in range(3)])
            em.op("vector", lambda e, jj=jj, j=j: e.tensor_tensor(
                out=act[:, jj, :], in0=sg[j % 2][:].rearrange("p a b -> p (a b)"), in1=tv[j % 2][:].rearrange("p a b -> p (a b)"), op=ALU.mult),
                reads=[(("tg", j % 2), tt) for tt in range(3)] + [(("tv", j % 2), tt) for tt in range(3)], writes=[("act", jj)])
        last = (g == NG - 1)
        for o2 in range(NK // 2):
            wd = wdl[o2 % 2]
            dkey = ("wdl", o2 % 2)
            em.op("gpsimd", lambda e, wd=wd, o2=o2, nj=nj, j0=j0: e.dma_start(
                out=wd[:, 0:nj, :], in_=w_downv[:, j0:j0 + nj, o2 * 256:(o2 + 1) * 256]), writes=[dkey], dma=dkey)
            for oo in range(2):
                o = o2 * 2 + oo
                xc = xcs[o % 2]
                xkey = ("xc", o % 2)
                if g == 0:
                    em.op("sync", lambda e, xc=xc, o=o: e.dma_start(out=xc[:, 0:T], in_=x1v[o][:, 2:NT]),
                          reads=[("x1d", o)], writes=[xkey], dma=xkey)
                else:
                    em.op("sync", lambda e, xc=xc, o=o: e.dma_start(out=xc[:, 0:T], in_=xov[o]),
                          reads=[("xo", o)], writes=[xkey], dma=xkey)
                for h in range(2):
                    b = down_banks[h]
                    pairs = [(wd[:, jj, oo * 128:(oo + 1) * 128], act[:, jj, h * TH:(h + 1) * TH]) for jj in range(nj)]
                    mm_group(em, psb[b][:, 0:TH], pairs, reads=[dkey] + [("act", jj) for jj in range(nj)], writes=[("ps", b)])
                    em.op("vector", lambda e, xc=xc, b=b, h=h: e.tensor_tensor(
                        out=xc[:, h * TH:(h + 1) * TH], in0=psb[b][:, 0:TH], in1=xc[:, h * TH:(h + 1) * TH], op=ALU.add),
                        reads=[("ps", b)], writes=[xkey])
                em.op("sync", lambda e, xc=xc, o=o: e.dma_start(out=xov[o], in_=xc[:, 0:T]),
                      reads=[xkey], writes=[("xo", o)], dma=xkey)
                if last:
                    sq = sqs[o % 2]
                    skey = ("sq", o % 2)
                    em.op("scalar", lambda e, sq=sq, xc=xc: e.activation(out=sq[:, 0:T], in_=xc[:, 0:T], func=AF.Square),
                          reads=[xkey], writes=[skey])
                    for h in range(2):
                        b = h
                        em.op("tensor", lambda e, b=b, sq=sq, h=h, o=o: e.matmul(
                            psb[b][:, 0:TH], ones[:], sq[:, h * TH:(h + 1) * TH], start=(o == 0), stop=(o == NK - 1)),
                            reads=[skey, "ones"], writes=[("ps", b)])
        j0 += nj
    emit_ssq_finish(cx, [psb[0][:, 0:TH], psb[1][:, 0:TH]], rstd2, [TH, TH], D, [("ps", 0), ("ps", 1)], "rstd2", sqs[0], "tmpr2")
    for k in range(NK):
        xc = xcs[k % 2]
        xkey = ("xc", k % 2)
        em.op("sync", lambda e, xc=xc, k=k: e.dma_start(out=xc[:, 0:T], in_=xov[k]), reads=[("xo", k)], writes=[xkey], dma=xkey)
        em.op("vector", lambda e, xc=xc, k=k: e.scalar_tensor_tensor(
            out=xc[:, 0:T], in0=xc[:, 0:T], scalar=NW(k), in1=rstd2[:, 0:T], op0=ALU.mult, op1=ALU.mult),
            reads=[xkey, "pv", ("rstd2", 0), ("rstd2", TH)], writes=[xkey])
        em.op("sync", lambda e, xc=xc, k=k: e.dma_start(out=hnov[k], in_=xc[:, 0:T]), reads=[xkey], writes=[("hno", k)], dma=xkey)
    return cx.done()


def pack_pvec_c(ln2, nw, cw, cb, D, DFF):
    NK, NJ = D // 128, DFF // 128
    pv = np.zeros((128, 2 * NK + 8 * NJ), np.float32)
    pv[:, 0:NK] = ln2.reshape(NK, 128).T
    pv[:, NK:2 * NK] = nw.reshape(NK, 128).T
    for tap in range(3):
        pv[:, 2 * NK + tap * 2 * NJ: 2 * NK + (tap + 1) * 2 * NJ] = cw[tap].reshape(2 * NJ, 128).T
    pv[:, 2 * NK + 6 * NJ:] = cb.reshape(2 * NJ, 128).T
    return pv


class Arena:
    def __init__(self, cx, name, nbytes):
        self.t = cx.sb(name, [128, nbytes // 4], F32)
        self.n = nbytes // 4
        self.off = 0

    def reset(self, off=0):
        self.off = off

    def f32(self, n, parts=128):
        a = self.t[0:parts, self.off:self.off + n]
        self.off += n
        assert self.off <= self.n, ("arena overflow", self.off, self.n)
        return a

    def bf16(self, n, parts=128):
        w = (n + 1) // 2
        a = self.t[0:parts, self.off:self.off + w].bitcast(BF16)
        self.off += w
        assert self.off <= self.n, ("arena overflow", self.off, self.n)
        return a[:, 0:n]


def barrier(em):
    waits = [(s, v) for s, v in em.val.items() if v > 0]
    for e in ENGS:
        ws = [(s, v) for s, v in waits if em.waited[e].get(s, 0) < v]
        for s, v in ws:
            em.waited[e][s] = v
        if ws:
            em.q[e].append((ws, [], None, 0))


def build_phase_a(D=D_MODEL, T=TOK):
    cx = Ctx()
    em = cx.em
    NK = D // 128
    TH = T // 2
    xT = cx.din("xT", [D, T], F32)
    pvec = cx.din("pvec", [128, NK], F32)
    hno = cx.dout("hno", [D, T], F32)
    pv = cx.sb("pv", [128, NK], F32)
    ones = cx.sb("ones", [128, 128], F32)
    cx.eps_t = cx.sb("eps_t", [128, 1], F32)
    xs = cx.sb("xs", [128, NK, T], F32)
    rstd = cx.sb("rstd", [128, T], F32)
    tmp = cx.sb("tmp", [128, T], F32)
    sqs = [cx.sb(f"sq{i}", [128, T], F32) for i in range(2)]
    psb = [cx.ps(f"psb{i}", [128, 512], F32) for i in range(2)]
    em.op("sync", lambda e: e.dma_start(out=pv[:], in_=pvec[:]), writes=["pv"], dma="pv")
    em.op("vector", lambda e: e.memset(ones[:], 1.0), writes=["ones"])
    em.op("vector", lambda e: e.memset(cx.eps_t[:], EPS), writes=["eps"])
    xv = xT.rearrange("(k p) t -> k p t", p=128)
    hv = hno.rearrange("(k p) t -> k p t", p=128)
    for k in range(NK):
        em.op("sync", lambda e, k=k: e.dma_start(out=xs[:, k, :], in_=xv[k]), writes=[("xs", k)], dma=("xs", k % 4))
        sq = sqs[k % 2]
        em.op("scalar", lambda e, sq=sq, k=k: e.activation(out=sq[:], in_=xs[:, k, :], func=AF.Square),
              reads=[("xs", k)], writes=[("sq", k % 2)])
        for h in range(2):
            em.op("tensor", lambda e, sq=sq, h=h, k=k: e.matmul(psb[h][:, 0:TH], ones[:], sq[:, h * TH:(h + 1) * TH],
                                                                start=(k == 0), stop=(k == NK - 1)),
                  reads=[("sq", k % 2), "ones"], writes=[("ps", h)])
    emit_ssq_finish(cx, [psb[0][:, 0:TH], psb[1][:, 0:TH]], rstd, [TH, TH], D, [("ps", 0), ("ps", 1)], "rstd", tmp, "tmp")
    for k in range(NK):
        em.op("vector", lambda e, k=k: e.scalar_tensor_tensor(
            out=xs[:, k, :], in0=xs[:, k, :], scalar=pv[:, k:k + 1], in1=rstd[:], op0=ALU.mult, op1=ALU.mult),
            reads=[("xs", k), "pv", ("rstd", 0), ("rstd", TH)], writes=[("xs", k)])
        em.op("sync", lambda e, k=k: e.dma_start(out=hv[k], in_=xs[:, k, :]), reads=[("xs", k)], writes=[("hno", k)], dma=("xs", k % 4))
    return cx.done()


NFM = 9
NTM = 386
NCOL = NFM * 128 + NTM
ARENA_BYTES = 200 * 1024


def build_phase_b(layer, D=D_MODEL, S=SEQ, NB=BATCH, do=("proj", "lru", "hg", "fox"), dbg=None):
    dbg = dbg or {}
    cx = Ctx()
    nc, em = cx.nc, cx.em
    NK = D // 128
    NTOK = NB * S
    NBLK = S // 128
    NCH = S // 64
    SCALE = 128 ** -0.5

    hnT = cx.din("hnT", [D, NTOK], F32)
    w_in = cx.din("w_in", [D, NCOL], F32)
    pvec = cx.din("pvec", [128, 16], F32)
    gw_d = cx.din("gw", [128, 256], F32)
    cmat_d = cx.din("cmat", [128, 256], F32)
    mixo = cx.dout("mixo", [512, NTOK], BF16)
    fm32 = cx.dint("fm32", [5, 128, NTOK], F32)
    fqk = cx.dint("fqk", [4, 128, NTOK], BF16)
    vtm = cx.dint("vtm", [NTOK, 2, 130], BF16)
    hgi = cx.dint("hgi", [NTOK, 128], BF16)

    pv = cx.sb("pv", [128, 16], F32)
    gw = cx.sb("gw_s", [128, 256], F32)
    ones = cx.sb("ones", [128, 128], F32)
    maskf = cx.sb("maskf", [128, 128], F32)
    maskb = cx.sb("maskb", [128, 128], BF16)
    identf = cx.sb("identf", [128, 128], F32)
    identb = cx.sb("identb", [128, 128], BF16)
    cst = cx.sb("cst", [128, 4], F32)
    cx.eps_t = cst
    sm = cx.sb("sm", [128, 16], F32)
    fxf = cx.sb("fxf", [128, NB * NBLK, 2], F32)
    ar = Arena(cx, "arena", ARENA_BYTES)
    psb = [cx.ps(f"psb{i}", [128, 512], F32) for i in range(8)]
    ONE = cst[:, 1:2]
    EPSA = cst[:, 0:1]

    em.op("sync", lambda e: e.dma_start(out=pv[:], in_=pvec[:]), writes=["pv"], dma="pv")
    em.op("sync", lambda e: e.dma_start(out=gw[:], in_=gw_d[:]), writes=["gw"], dma="gw")
    em.op("vector", lambda e: e.memset(ones[:], 1.0), writes=["ones"])
    em.op("vector", lambda e: e.memset(cst[:, 0:1], EPS), writes=["cst0"])
    em.op("vector", lambda e: e.memset(cst[:, 1:2], 1.0), writes=["cst1"])
    em.op("vector", lambda e: e.memset(cst[:, 2:3], 0.0), writes=["cst2"])
    em.op("vector", lambda e: e.memset(cst[:, 3:4], -1.0), writes=["cst3"])
    em.op("sync", lambda e: e.dma_start(out=maskf[:], in_=cmat_d[:, 0:128]), writes=["maskf"], dma="maskf")
    em.op("sync", lambda e: e.dma_start(out=identf[:], in_=cmat_d[:, 128:256]), writes=["identf"], dma="identf")
    em.op("vector", lambda e: e.tensor_copy(out=maskb[:], in_=maskf[:]), reads=["maskf"], writes=["maskb"])
    em.op("vector", lambda e: e.tensor_copy(out=identb[:], in_=identf[:]), reads=["identf"], writes=["identb"])
    barrier(em)

    PVc = lambda i: pv[:, i:i + 1]
    hnv = hnT.rearrange("(k p) t -> p k t", p=128)
    wv = w_in.rearrange("(k p) c -> p k c", p=128)

    if "proj" in do:
        ar.reset()
        wsb = ar.bf16(NK * NCOL).rearrange("p (k c) -> p k c", c=NCOL)
        hns = [ar.bf16(NK * 512).rearrange("p (k t) -> p k t", t=512) for _ in range(2)]
        st32 = [ar.f32(512) for _ in range(3)]
        st16 = [ar.bf16(512) for _ in range(2)]
        stv = [ar.bf16(4 * 260).rearrange("p (n c) -> p n c", c=260) for _ in range(2)]
        sti = [ar.bf16(4 * 128).rearrange("p (n c) -> p n c", c=128) for _ in range(2)]
        for i in range(2):
            v4 = stv[i].rearrange("p n (h c) -> p n h c", c=130)
            for hh in range(2):
                em.op("vector", lambda e, v4=v4, hh=hh: e.memset(v4[:, :, hh, 128:130], 1.0), writes=[("stv1", i)])
        for k in range(NK):
            em.op("gpsimd", lambda e, k=k: e.dma_start(out=wsb[:, k, :], in_=wv[:, k, :]), writes=[("wsb", k)], dma=("wsb", k % 4))
        wkeys = [("wsb", k) for k in range(NK)]
        vtmv = vtm.rearrange("(n p) h c -> p n (h c)", p=128)
        hgiv = hgi.rearrange("(n p) c -> p n c", p=128)
        n32 = n16 = 0
        bankc = 0
        for ti in range(dbg.get("ntiles", NTOK // 512)):
            hs = hns[ti % 2]
            hkey = ("hns", ti % 2)
            t0 = ti * 512
            em.op("gpsimd", lambda e, hs=hs, t0=t0: e.dma_start(out=hs[:], in_=hnv[:, :, t0:t0 + 512]), writes=[hkey], dma=hkey)
            for c in range(0 if not dbg.get("skip_fm") else NFM, NFM):
                bank = bankc % 4
                bankc += 1
                pairs = [(wsb[:, k, c * 128:(c + 1) * 128], hs[:, k, :]) for k in range(NK)]
                mm_group(em, psb[bank][:, 0:512], pairs, reads=[hkey] + wkeys, writes=[("ps", bank)])
                if c < 5:
                    st = st32[n32 % 3]
                    skey = ("st32", n32 % 3)
                    n32 += 1
                    func = {2: AF.Silu, 4: AF.Sigmoid}.get(c)
                    if func is not None or c == 0:
                        f2 = func if func is not None else AF.Copy
                        em.op("scalar", lambda e, st=st, bank=bank, f2=f2: e.activation(out=st, in_=psb[bank][:, 0:512], func=f2),
                              reads=[("ps", bank)], writes=[skey])
                    else:
                        em.op("vector", lambda e, st=st, bank=bank: e.tensor_copy(out=st, in_=psb[bank][:, 0:512]),
                              reads=[("ps", bank)], writes=[skey])
                    em.op("sync", lambda e, st=st, c=c, t0=t0: e.dma_start(out=fm32[c][:, t0:t0 + 512], in_=st),
                          reads=[skey], writes=[("fm32", c, ti)], dma=skey)
                else:
                    st = st16[n16 % 2]
                    skey = ("st16", n16 % 2)
                    n16 += 1
                    if c % 2 == 0:
                        em.op("scalar", lambda e, st=st, bank=bank: e.activation(out=st, in_=psb[bank][:, 0:512], func=AF.Copy),
                              reads=[("ps", bank)], writes=[skey])
                    else:
                        em.op("vector", lambda e, st=st, bank=bank: e.tensor_copy(out=st, in_=psb[bank][:, 0:512]),
                              reads=[("ps", bank)], writes=[skey])
                    em.op("sync", lambda e, st=st, c=c, t0=t0: e.dma_start(out=fqk[c - 5][:, t0:t0 + 512], in_=st),
                          reads=[skey], writes=[("fqk", c - 5, ti)], dma=skey)
            sv = stv[ti % 2]
            si = sti[ti % 2]
            if dbg.get("skip_tm"):
                continue
            for sub in range(4):
                bank = 4 + (ti * 4 + sub) % 2
                pairs = [(hs[:, k, sub * 128:(sub + 1) * 128], wsb[:, k, NFM * 128:NCOL]) for k in range(NK)]
                mm_group(em, psb[bank][:, 0:NTM], pairs, reads=[hkey] + wkeys, writes=[("ps", bank)])
                sv4 = sv.rearrange("p n (h c) -> p n h c", c=130)
                if dbg.get("skip_tm_evac"):
                    em.op("vector", lambda e, si=si, bank=bank, sub=sub: e.tensor_copy(out=si[:, sub, :], in_=psb[bank][:, 258:386]),
                          reads=[("ps", bank)], writes=[("sti", ti % 2, sub)])
                    continue
                em.op("vector" if True else "scalar", lambda e, sv4=sv4, bank=bank, sub=sub: e.tensor_copy(
                    out=sv4[:, sub, 0, 0:128], in_=psb[bank][:, 0:128]) if True else e.activation(
                    out=sv4[:, sub, 0, 0:128], in_=psb[bank][:, 0:128], func=AF.Copy),
                    reads=[("ps", bank), ("stv1", ti % 2)], writes=[("stv", ti % 2, sub)])
                em.op("vector", lambda e, sv4=sv4, bank=bank, sub=sub: e.tensor_copy(
                    out=sv4[:, sub, 1, 0:128], in_=psb[bank][:, 128:256]),
                    reads=[("ps", bank), ("stv1", ti % 2)], writes=[("stv", ti % 2, sub, 1)])
                em.op("vector", lambda e, si=si, bank=bank, sub=sub: e.tensor_copy(out=si[:, sub, :], in_=psb[bank][:, 258:386]),
                      reads=[("ps", bank)], writes=[("sti", ti % 2, sub)])
                blk = ti * 4 + sub
                if dbg.get("no_fxf"):
                    continue
                em.op("vector", lambda e, bank=bank, blk=blk: e.tensor_copy(out=fxf[:, blk, :], in_=psb[bank][:, 256:258]),
                      reads=[("ps", bank)], writes=[("fxf", blk)])
            if dbg.get("skip_tm_dma"):
                continue
            em.op("sync", lambda e, sv=sv, ti=ti: e.dma_start(out=vtmv[:, ti * 4:(ti + 1) * 4, :], in_=sv),
                  reads=[("stv", ti % 2, s_) for s_ in range(4)] + [("stv", ti % 2, s_, 1) for s_ in range(4)], writes=[("vtm", ti)], dma=("stv", ti % 2))
            em.op("sync", lambda e, si=si, ti=ti: e.dma_start(out=hgiv[:, ti * 4:(ti + 1) * 4, :], in_=si),
                  reads=[("sti", ti % 2, s_) for s_ in range(4)], writes=[("hgi", ti)], dma=("sti", ti % 2))
        barrier(em)

    if "lru" in do:
        em.op("scalar", lambda e: e.activation(out=sm[:, 0:1], in_=PVc(7), func=AF.Exp, scale=-1.0), reads=["pv"], writes=["sm0"])
        em.op("scalar", lambda e: e.activation(out=sm[:, 1:2], in_=sm[:, 0:1], func=AF.Ln, bias=ONE), reads=["sm0", "cst1"], writes=["sm1"])
        em.op("vector", lambda e: e.tensor_scalar(out=sm[:, 2:3], in0=sm[:, 1:2], scalar1=-8.0, scalar2=None, op0=ALU.mult), reads=["sm1"], writes=["sm2"])
        em.op("vector", lambda e: e.tensor_scalar(out=sm[:, 3:4], in0=sm[:, 1:2], scalar1=-16.0, scalar2=None, op0=ALU.mult), reads=["sm1"], writes=["sm3"])
        C8, C16 = sm[:, 2:3], sm[:, 3:4]
        for b in range(NB):
            ar.reset()
            B = [ar.f32(S) for _ in range(7)]
            ob = ar.bf16(S)
            L = lambda i: ("L", i)
            t0 = b * S
            em.op("sync", lambda e, t0=t0: e.dma_start(out=B[0], in_=fm32[0][:, t0:t0 + S]), writes=[L(0)], dma="L0")
            em.op("sync", lambda e, t0=t0: e.dma_start(out=B[6], in_=fm32[1][:, t0:t0 + S]), writes=[L(6)], dma="L6")
            em.op("scalar", lambda e: e.activation(out=B[1], in_=B[0], func=AF.Identity, bias=PVc(4), scale=PVc(3)),
                  reads=[L(0), "pv"], writes=[L(1)])
            for sh, tap in ((1, 2), (2, 1), (3, 0)):
                em.op("vector", lambda e, sh=sh, tap=tap: e.scalar_tensor_tensor(
                    out=B[1][:, sh:S], in0=B[0][:, 0:S - sh], scalar=PVc(tap), in1=B[1][:, sh:S], op0=ALU.mult, op1=ALU.add),
                    reads=[L(0), L(1), "pv"], writes=[L(1)])
            for i in range(S // 512):
                for gi, (dst, bcol) in enumerate(((2, 5), (3, 6))):
                    bank = (i * 2 + gi) % 4
                    em.op("tensor", lambda e, bank=bank, gi=gi, i=i: e.matmul(
                        psb[bank][:, 0:512], gw[:, gi * 128:(gi + 1) * 128], B[1][:, i * 512:(i + 1) * 512], start=True, stop=True),
                        reads=["gw", L(1)], writes=[("ps", bank)])
                    em.op("scalar", lambda e, bank=bank, dst=dst, bcol=bcol, i=i: e.activation(
                        out=B[dst][:, i * 512:(i + 1) * 512], in_=psb[bank][:, 0:512], func=AF.Sigmoid, bias=PVc(bcol)),
                        reads=[("ps", bank), "pv"], writes=[L(dst)])
            em.op("scalar", lambda e: e.activation(out=B[4], in_=B[2], func=AF.Exp, scale=C8), reads=[L(2), "sm2"], writes=[L(4)])
            em.op("scalar", lambda e: e.activation(out=B[0], in_=B[2], func=AF.Exp, scale=C16), reads=[L(2), "sm3"], writes=[L(0)])
            em.op("scalar", lambda e: e.activation(out=B[0], in_=B[0], func=AF.Sqrt, bias=ONE, scale=-1.0), reads=[L(0), "cst1"], writes=[L(0)])
            em.op("vector", lambda e: e.tensor_tensor(out=B[3], in0=B[3], in1=B[1], op=ALU.mult), reads=[L(3), L(1)], writes=[L(3)])
            em.op("vector", lambda e: e.tensor_tensor(out=B[3], in0=B[3], in1=B[0], op=ALU.mult), reads=[L(3), L(0)], writes=[L(3)])
            em.op("vector", lambda e: e.tensor_tensor_scan(out=B[2], data0=B[4], data1=B[3], initial=0.0, op0=ALU.mult, op1=ALU.add),
                  reads=[L(4), L(3), L(2)], writes=[L(2)])
            em.op("scalar", lambda e: e.activation(out=B[0], in_=B[2], func=AF.Square), reads=[L(2)], writes=[L(0)])
            for i in range(S // 512):
                bank = i % 4
                em.op("tensor", lambda e, bank=bank, i=i: e.matmul(psb[bank][:, 0:512], ones[:], B[0][:, i * 512:(i + 1) * 512], start=True, stop=True),
                      reads=["ones", L(0)], writes=[("ps", bank)])
                em.op("scalar", lambda e, bank=bank, i=i: e.activation(out=B[5][:, i * 512:(i + 1) * 512], in_=psb[bank][:, 0:512],
                                                                       func=AF.Sqrt, bias=EPSA, scale=1.0 / 128),
                      reads=[("ps", bank), "cst0"], writes=[L(5)])
            em.op("vector", lambda e: e.reciprocal(out=B[5], in_=B[5]), reads=[L(5)], writes=[L(5)])
            em.op("scalar", lambda e: e.activation(out=B[0], in_=B[6], func=AF.Square), reads=[L(6), L(0)], writes=[L(0)])
            em.op("vector", lambda e: e.tensor_scalar(out=B[0], in0=B[0], scalar1=0.044715, scalar2=1.0, op0=ALU.mult, op1=ALU.add),
                  reads=[L(0)], writes=[L(0)])
            em.op("vector", lambda e: e.tensor_tensor(out=B[0], in0=B[0], in1=B[6], op=ALU.mult), reads=[L(0), L(6)], writes=[L(0)])
            em.op("scalar", lambda e: e.activation(out=B[0], in_=B[0], func=AF.Sigmoid, scale=1.5957691216057308), reads=[L(0)], writes=[L(0)])
            em.op("vector", lambda e: e.tensor_tensor(out=B[6], in0=B[6], in1=B[0], op=ALU.mult), reads=[L(0), L(6)], writes=[L(6)])
            em.op("vector", lambda e: e.scalar_tensor_tensor(out=B[2], in0=B[2], scalar=PVc(10), in1=B[5], op0=ALU.mult, op1=ALU.mult),
                  reads=[L(2), L(5), "pv"], writes=[L(2)])
            em.op("vector", lambda e: e.tensor_tensor(out=ob, in0=B[2], in1=B[6], op=ALU.mult), reads=[L(2), L(6)], writes=["Lob"])
            em.op("sync", lambda e, t0=t0: e.dma_start(out=mixo[0:128, t0:t0 + S], in_=ob), reads=["Lob"], writes=[("mixo", 0, b)], dma="Lob")
            barrier(em)

    if "hg" in do:
        if layer == 0:
            LB, OML, NOML = cst[:, 2:3], cst[:, 1:2], cst[:, 3:4]
            lbkeys = ["cst1", "cst2", "cst3"]
        else:
            em.op("vector", lambda e: e.tensor_tensor(out=sm[:, 4:5], in0=PVc(9), in1=PVc(8), op=ALU.subtract), reads=["pv"], writes=["sm4"])
            em.op("scalar", lambda e: e.activation(out=sm[:, 5:6], in_=sm[:, 4:5], func=AF.Sigmoid), reads=["sm4"], writes=["sm5"])
            em.op("vector", lambda e: e.tensor_scalar(out=sm[:, 6:7], in0=sm[:, 5:6], scalar1=-1.0, scalar2=1.0, op0=ALU.mult, op1=ALU.add), reads=["sm5"], writes=["sm6"])
            em.op("vector", lambda e: e.tensor_scalar(out=sm[:, 7:8], in0=sm[:, 6:7], scalar1=-1.0, scalar2=None, op0=ALU.mult), reads=["sm6"], writes=["sm7"])
            LB, OML, NOML = sm[:, 5:6], sm[:, 6:7], sm[:, 7:8]
            lbkeys = ["sm5", "sm6", "sm7"]
        ar.reset()
        rmask = ar.f32(S)
        base_off = ar.off
        em.op("vector", lambda e: e.memset(rmask, 1.0), writes=["rmask"])
        em.op("vector", lambda e: e.memset(rmask.rearrange("p (c s) -> p c s", s=64)[:, :, 0:1], 0.0), writes=["rmask"])
        for b in range(NB):
            ar.reset(base_off)
            Q, Z, Fb, CUM = [ar.f32(S) for _ in range(4)]
            qd, kd, qe, kl = [ar.bf16(S) for _ in range(4)]
            Sbf = ar.bf16(NCH * 128)
            kltm = ar.bf16(NCH * 128, parts=64)
            vt = ar.bf16(NCH * 128, parts=64)
            scT = ar.bf16(NCH * 64, parts=64)
            ob = ar.bf16(S)
            st = [ar.f32(128) for _ in range(2)]
            cmid, last, ecm, elc, elast, dl = [ar.f32(NCH) for _ in range(6)]
            t0 = b * S
            H = lambda n: ("H", n)
            c3 = lambda a: a.rearrange("p (c s) -> p c s", s=64)
            em.op("sync", lambda e, t0=t0: e.dma_start(out=Q, in_=fm32[2][:, t0:t0 + S]), writes=[H("Q")], dma="HQ")
            em.op("sync", lambda e, t0=t0: e.dma_start(out=Z, in_=fm32[3][:, t0:t0 + S]), writes=[H("Z")], dma="HZ")
            hgiv2 = hgi.rearrange("(c s) v -> s c v", s=64)
            em.op("sync", lambda e, b=b: e.dma_start(out=vt.rearrange("s (c v) -> s c v", v=128), in_=hgiv2[:, b * NCH:(b + 1) * NCH, :]),
                  writes=[H("vt")], dma="Hvt")
            em.op("scalar", lambda e: e.activation(out=Z, in_=Z, func=AF.Sigmoid), reads=[H("Z")], writes=[H("Z")])
            em.op("vector", lambda e: e.tensor_scalar(out=Fb, in0=Z, scalar1=OML, scalar2=LB, op0=ALU.mult, op1=ALU.add),
                  reads=[H("Z")] + lbkeys, writes=[H("F")])
            em.op("scalar", lambda e: e.activation(out=Fb, in_=Fb, func=AF.Ln), reads=[H("F")], writes=[H("F")])
            em.op("vector", lambda e: e.tensor_scalar(out=Z, in0=Z, scalar1=NOML, scalar2=OML, op0=ALU.mult, op1=ALU.add),
                  reads=[H("Z"), H("F")] + lbkeys, writes=[H("Z")])
            em.op("vector", lambda e: e.tensor_tensor_scan(out=CUM, data0=rmask, data1=Fb, initial=0.0, op0=ALU.mult, op1=ALU.add),
                  reads=["rmask", H("F")], writes=[H("C")])
            em.op("vector", lambda e: e.tensor_copy(out=cmid.unsqueeze(2), in_=c3(CUM)[:, :, 31:32]), reads=[H("C")], writes=[H("cmid")])
            em.op("vector", lambda e: e.tensor_copy(out=last.unsqueeze(2), in_=c3(CUM)[:, :, 63:64]), reads=[H("C")], writes=[H("last")])
            em.op("scalar", lambda e: e.activation(out=ecm, in_=cmid, func=AF.Exp), reads=[H("cmid")], writes=[H("ecm")])
            em.op("vector", lambda e: e.tensor_tensor(out=dl, in0=last, in1=cmid, op=ALU.subtract), reads=[H("cmid"), H("last")], writes=[H("dl")])
            em.op("scalar", lambda e: e.activation(out=elc, in_=dl, func=AF.Exp), reads=[H("dl")], writes=[H("elc")])
            em.op("scalar", lambda e: e.activation(out=elast, in_=last, func=AF.Exp), reads=[H("last")], writes=[H("elast")])
            em.op("vector", lambda e: e.tensor_tensor(out=c3(CUM), in0=c3(CUM), in1=cmid.unsqueeze(2).broadcast_to([128, NCH, 64]), op=ALU.subtract),
                  reads=[H("C"), H("cmid")], writes=[H("C")])
            em.op("scalar", lambda e: e.activation(out=Fb, in_=CUM, func=AF.Exp), reads=[H("C"), H("F")], writes=[H("F")])
            em.op("scalar", lambda e: e.activation(out=CUM, in_=CUM, func=AF.Exp, scale=-1.0), reads=[H("C")], writes=[H("C")])
            em.op("vector", lambda e: e.tensor_tensor(out=Q, in0=Q, in1=Fb, op=ALU.mult), reads=[H("Q"), H("F")], writes=[H("Q")])
            em.op("vector", lambda e: e.tensor_tensor(out=Z, in0=Z, in1=CUM, op=ALU.mult), reads=[H("Z"), H("C")], writes=[H("Z")])
            em.op("scalar", lambda e: e.activation(out=qd, in_=Q, func=AF.Copy), reads=[H("Q")], writes=[H("qd")])
            em.op("scalar", lambda e: e.activation(out=kd, in_=Z, func=AF.Copy), reads=[H("Z")], writes=[H("kd")])
            em.op("vector", lambda e: e.tensor_tensor(out=c3(qe), in0=c3(Q), in1=ecm.unsqueeze(2).broadcast_to([128, NCH, 64]), op=ALU.mult),
                  reads=[H("Q"), H("ecm")], writes=[H("qe")])
            em.op("vector", lambda e: e.tensor_tensor(out=c3(kl), in0=c3(Z), in1=elc.unsqueeze(2).broadcast_to([128, NCH, 64]), op=ALU.mult),
                  reads=[H("Z"), H("elc")], writes=[H("kl")])
            for g in range(NCH // 8):
                bank = g % 2
                pT = psb[bank][:, 0:512].bitcast(BF16)
                fns = [lambda e, pT=pT, c=c, g=g: e.transpose(out=pT[0:64, (c % 8) * 128:(c % 8 + 1) * 128], in_=kl[:, c * 64:(c + 1) * 64], identity=identb[:])
                       for c in range(g * 8, g * 8 + 8)]
                em.op("tensor", fns, reads=[H("kl"), "identb"], writes=[("ps", bank)])
                em.op("scalar", lambda e, pT=pT, g=g: e.activation(out=kltm[:, g * 1024:(g + 1) * 1024], in_=pT[0:64, 0:1024], func=AF.Copy),
                      reads=[("ps", bank)], writes=[H(("kltm", g))])
            em.op("gpsimd", lambda e: e.memset(Sbf[:, 0:128], 0.0), writes=[H(("S", 0))])
            for g in range(NCH // 4):
                bank = 2 + g % 2
                fns = [lambda e, c=c, bank=bank: e.matmul(psb[bank][:, (c % 4) * 128:(c % 4 + 1) * 128], kltm[:, c * 128:(c + 1) * 128],
                                                          vt[:, c * 128:(c + 1) * 128], start=True, stop=True) for c in range(g * 4, g * 4 + 4)]
                em.op("tensor", fns, reads=[H(("kltm", g // 2)), H("vt")], writes=[("ps", bank)])
                for c in range(g * 4, g * 4 + 4):
                    if c == NCH - 1:
                        break
                    kvv = psb[bank][:, (c % 4) * 128:(c % 4 + 1) * 128]
                    sn = st[(c + 1) % 2]
                    so = st[c % 2]
                    if c == 0:
                        em.op("vector", lambda e, kvv=kvv, sn=sn: e.tensor_copy(out=sn, in_=kvv), reads=[("ps", bank)], writes=[H(("st", 1))])
                    else:
                        em.op("vector", lambda e, kvv=kvv, sn=sn, so=so, c=c: e.scalar_tensor_tensor(
                            out=sn, in0=so, scalar=elast[:, c:c + 1], in1=kvv, op0=ALU.mult, op1=ALU.add),
                            reads=[("ps", bank), H(("st", c % 2)), H("elast")], writes=[H(("st", (c + 1) % 2))])
                    em.op("scalar", lambda e, sn=sn, c=c: e.activation(out=Sbf[:, (c + 1) * 128:(c + 2) * 128], in_=sn, func=AF.Copy),
                          reads=[H(("st", (c + 1) % 2))], writes=[H(("S", c + 1))])
            for g in range(NCH // 8):
                bank = 4 + g % 2
                fns = [lambda e, c=c, bank=bank: e.matmul(psb[bank][0:64, (c % 8) * 64:(c % 8 + 1) * 64], kd[:, c * 64:(c + 1) * 64],
                                                          qd[:, c * 64:(c + 1) * 64], start=True, stop=True) for c in range(g * 8, g * 8 + 8)]
                em.op("tensor", fns, reads=[H("kd"), H("qd")], writes=[("ps", bank)])
                em.op("vector", lambda e, bank=bank, g=g: e.tensor_tensor(
                    out=scT[:, g * 512:(g + 1) * 512].rearrange("s (c t) -> s c t", t=64),
                    in0=psb[bank][0:64, 0:512].rearrange("s (c t) -> s c t", t=64),
                    in1=maskf[0:64, 0:64].unsqueeze(1).broadcast_to([64, 8, 64]), op=ALU.mult),
                    reads=[("ps", bank), "maskf"], writes=[H(("sc", g))])
            for g in range(NCH // 8):
                bank = 6 + g % 2
                fns = []
                for c in range(g * 8, g * 8 + 8):
                    o_ap = psb[bank][:, (c % 8) * 64:(c % 8 + 1) * 64]
                    fns.append(lambda e, c=c, o_ap=o_ap: e.matmul(o_ap, vt[:, c * 128:(c + 1) * 128], scT[:, c * 64:(c + 1) * 64], start=True, stop=False))
                    fns.append(lambda e, c=c, o_ap=o_ap: e.matmul(o_ap, Sbf[:, c * 128:(c + 1) * 128], qe[:, c * 64:(c + 1) * 64], start=False, stop=True))
                em.op("tensor", fns, reads=[H("vt"), H(("sc", g)), H("qe")] + [H(("S", c)) for c in range(g * 8, g * 8 + 8)], writes=[("ps", bank)])
                em.op("scalar", lambda e, bank=bank, g=g: e.activation(out=CUM[:, g * 512:(g + 1) * 512], in_=psb[bank][:, 0:512], func=AF.Copy),
                      reads=[("ps", bank), H("C")], writes=[H("O")])
            em.op("scalar", lambda e: e.activation(out=Fb, in_=CUM, func=AF.Square), reads=[H("O"), H("F")], writes=[H("F")])
            for i in range(S // 512):
                bank = i % 2
                em.op("tensor", lambda e, bank=bank, i=i: e.matmul(psb[bank][:, 0:512], ones[:], Fb[:, i * 512:(i + 1) * 512], start=True, stop=True),
                      reads=["ones", H("F")], writes=[("ps", bank)])
                em.op("scalar", lambda e, bank=bank, i=i: e.activation(out=Q[:, i * 512:(i + 1) * 512], in_=psb[bank][:, 0:512],
                                                                       func=AF.Sqrt, bias=EPSA, scale=1.0 / 128),
                      reads=[("ps", bank), "cst0", H("qd"), H("qe")], writes=[H("Q")])
            em.op("vector", lambda e: e.reciprocal(out=Q, in_=Q), reads=[H("Q")], writes=[H("Q")])
            em.op("sync", lambda e, t0=t0: e.dma_start(out=Z, in_=fm32[4][:, t0:t0 + S]), reads=[H("kd"), H("kl")], writes=[H("Z")], dma="HZ")
            em.op("vector", lambda e: e.scalar_tensor_tensor(out=CUM, in0=CUM, scalar=PVc(11), in1=Q, op0=ALU.mult, op1=ALU.mult),
                  reads=[H("O"), H("Q"), "pv"], writes=[H("O")])
            em.op("vector", lambda e: e.tensor_tensor(out=ob, in0=CUM, in1=Z, op=ALU.mult), reads=[H("O"), H("Z")], writes=[H("ob")])
            em.op("sync", lambda e, t0=t0: e.dma_start(out=mixo[128:256, t0:t0 + S], in_=ob), reads=[H("ob")], writes=[("mixo", 1, b)], dma="Hob")
            barrier(em)

    if "fox" in do:
        vtm4 = vtm.rearrange("(n p) h c -> p n h c", p=128)
        em.op("vector", lambda e: e.tensor_scalar(out=sm[:, 8:10], in0=pv[:, 14:16], scalar1=-1.0, scalar2=None, op0=ALU.mult), reads=["pv"], writes=["sm8"])
        for b in range(NB):
            for h in range(2):
                ar.reset()
                qT = ar.bf16(S)
                kT = ar.bf16(S)
                va = ar.bf16(NBLK * 130)
                obuf = ar.bf16(S)
                bt = ar.f32(NBLK * NBLK)
                nlf, Wsb, Tt, pinc, pe, ncum, ones32 = [ar.f32(NBLK) for _ in range(7)]
                pts = [ar.bf16(128) for _ in range(4)]
                ost = [ar.f32(128) for _ in range(2)]
                junk = ar.f32(128)
                smt = [ar.f32(4) for _ in range(2)]
                t0 = b * S
                X = lambda n: ("X", n)
                em.op("sync", lambda e, t0=t0, h=h: e.dma_start(out=qT, in_=fqk[h][:, t0:t0 + S]), writes=[X("q")], dma="Xq")
                em.op("sync", lambda e, t0=t0, h=h: e.dma_start(out=kT, in_=fqk[2 + h][:, t0:t0 + S]), writes=[X("k")], dma="Xk")
                em.op("sync", lambda e, b=b, h=h: e.dma_start(out=va.rearrange("p (n c) -> p n c", c=130), in_=vtm4[:, b * NBLK:(b + 1) * NBLK, h, :]),
                      writes=[X("v")], dma="Xv")
                ffv = fxf[:, b * NBLK:(b + 1) * NBLK, h:h + 1]
                em.op("scalar", lambda e, ffv=ffv, h=h: e.activation(out=nlf.unsqueeze(2), in_=ffv, func=AF.Exp, bias=sm[:, 8 + h:9 + h], scale=-1.0),
                      reads=["sm8"] + [("fxf", blk) for blk in range(b * NBLK, (b + 1) * NBLK)], writes=[X("nlf")])
                em.op("scalar", lambda e: e.activation(out=nlf, in_=nlf, func=AF.Ln, bias=ONE), reads=[X("nlf"), "cst1"], writes=[X("nlf")])
                em.op("vector", lambda e: e.memset(ones32, 1.0), writes=[X("ones32")])
                em.op("tensor", [lambda e: e.matmul(psb[7][:, 0:NBLK], maskf[:], nlf, start=True, stop=True),
                                 lambda e: e.matmul(psb[7][:, NBLK:2 * NBLK], ones[:], nlf, start=True, stop=True)],
                      reads=[X("nlf"), "maskf", "ones"], writes=[("ps", 7)])
                em.op("vector", lambda e: e.tensor_copy(out=Wsb, in_=psb[7][:, 0:NBLK]), reads=[("ps", 7)], writes=[X("W")])
                em.op("vector", lambda e: e.tensor_copy(out=Tt, in_=psb[7][:, NBLK:2 * NBLK]), reads=[("ps", 7)], writes=[X("Tt")])
                em.op("vector", lambda e: e.tensor_tensor_scan(out=pinc, data0=ones32, data1=Tt, initial=0.0, op0=ALU.mult, op1=ALU.add),
                      reads=[X("ones32"), X("Tt")], writes=[X("pinc")])
                em.op("vector", lambda e: e.tensor_tensor(out=pe, in0=pinc, in1=Tt, op=ALU.subtract), reads=[X("pinc"), X("Tt")], writes=[X("pe")])
                em.op("vector", lambda e: e.tensor_tensor(out=ncum, in0=Wsb, in1=pe, op=ALU.add), reads=[X("W"), X("pe")], writes=[X("ncum")])
                em.op("vector", lambda e: e.tensor_tensor(
                    out=bt.rearrange("p (s t) -> p s t", t=NBLK), in0=ncum.unsqueeze(2).broadcast_to([128, NBLK, NBLK]),
                    in1=pe.unsqueeze(1).broadcast_to([128, NBLK, NBLK]), op=ALU.subtract),
                    reads=[X("ncum"), X("pe")], writes=[X("bt")])
                npt = 0
                nsc = 0
                for tg in range(NBLK // 4):
                    for sb in range(4 * tg + 4):
                        t_lo = max(4 * tg, sb)
                        ncols = (4 * tg + 4 - t_lo) * 128
                        sbank = nsc % 3
                        nsc += 1
                        em.op("tensor", lambda e, sbank=sbank, sb=sb, t_lo=t_lo, ncols=ncols: e.matmul(
                            psb[sbank][:, 0:ncols], kT[:, sb * 128:(sb + 1) * 128], qT[:, t_lo * 128:t_lo * 128 + ncols], start=True, stop=True),
                            reads=[X("q"), X("k")], writes=[("ps", sbank)])
                        for tb in range(t_lo, 4 * tg + 4):
                            pt = pts[npt % 4]
                            pkey = X(("pt", npt % 4))
                            npt += 1
                            off = (tb - t_lo) * 128
                            em.op("scalar", lambda e, pt=pt, sbank=sbank, off=off, sb=sb, tb=tb: e.activation(
                                out=pt, in_=psb[sbank][:, off:off + 128], func=AF.Exp, bias=bt[:, sb * NBLK + tb:sb * NBLK + tb + 1], scale=SCALE),
                                reads=[("ps", sbank), X("bt")], writes=[pkey])
                            if sb == tb:
                                em.op("vector", lambda e, pt=pt: e.tensor_tensor(out=pt, in0=pt, in1=maskb[:], op=ALU.mult),
                                      reads=[pkey, "maskb"], writes=[pkey])
                            abank = 3 + (tb - 4 * tg)
                            em.op("tensor", lambda e, abank=abank, pt=pt, sb=sb, tb=tb: e.matmul(
                                psb[abank][:, 0:129], pt, va[:, sb * 130:sb * 130 + 129], start=(sb == 0), stop=(sb == tb)),
                                reads=[pkey, X("v")], writes=[("ps", abank)])
                            if sb == tb:
                                i2 = tb % 2
                                os_ = ost[i2]
                                sc4 = smt[i2]
                                fk = X(("fin", i2))
                                em.op("vector", lambda e, abank=abank, sc4=sc4: e.reciprocal(out=sc4[:, 0:1], in_=psb[abank][:, 128:129]),
                                      reads=[("ps", abank)], writes=[fk])
                                em.op("vector", lambda e, abank=abank, os_=os_, sc4=sc4: e.tensor_scalar(
                                    out=os_, in0=psb[abank][:, 0:128], scalar1=sc4[:, 0:1], scalar2=None, op0=ALU.mult),
                                    reads=[("ps", abank), fk], writes=[fk])
                                em.op("scalar", lambda e, os_=os_, sc4=sc4: e.activation(out=junk, in_=os_, func=AF.Square, accum_out=sc4[:, 1:2]),
                                      reads=[fk], writes=[fk, X("junk")])
                                em.op("scalar", lambda e, sc4=sc4: e.activation(out=sc4[:, 2:3], in_=sc4[:, 1:2], func=AF.Sqrt, bias=EPSA, scale=1.0 / 128),
                                      reads=[fk, "cst0"], writes=[fk])
                                em.op("vector", lambda e, sc4=sc4: e.reciprocal(out=sc4[:, 3:4], in_=sc4[:, 2:3]), reads=[fk], writes=[fk])
                                em.op("vector", lambda e, os_=os_, sc4=sc4: e.tensor_scalar(out=os_, in0=os_, scalar1=sc4[:, 3:4], scalar2=None, op0=ALU.mult),
                                      reads=[fk], writes=[fk])
                                em.op("tensor", lambda e, os_=os_: e.transpose(out=psb[7][:, 0:128], in_=os_, identity=identf[:]),
                                      reads=[fk, "identf"], writes=[("ps", 7)])
                                em.op("scalar", lambda e, tb=tb, h=h: e.activation(out=obuf[:, tb * 128:(tb + 1) * 128], in_=psb[7][:, 0:128],
                                                                                func=AF.Identity, scale=PVc(12 + h)),
                                      reads=[("ps", 7), "pv"], writes=[X("obuf")])
                em.op("sync", lambda e, t0=t0, h=h: e.dma_start(out=mixo[256 + h * 128:384 + h * 128, t0:t0 + S], in_=obuf),
                      reads=[X("obuf")], writes=[("mixo", 2 + h, b)], dma="Xob")
                barrier(em)
    return cx.done()


def make_cmat():
    c = np.zeros((128, 256), np.float32)
    p = np.arange(128)
    c[:, 0:128] = (p[None, :] >= p[:, None]).astype(np.float32)
    c[:, 128:256] = np.eye(128, dtype=np.float32)
    return c


LRU_W = 1024
HG_W = 1024
FOX_W = 2048
OFF = {"lru_x": 0, "lru_y": 1024, "hg_q": 2048, "hg_f": 3072, "hg_i": 4096, "hg_g": 5120,
       "fx_q": 6144, "fx_k": 8192, "fx_v": 10240, "fx_f": 12288}


def _w_in_cols(c):
    cols = []
    for nm in ("lru_x", "lru_y", "hg_q", "hg_f", "hg_g"):
        cols.append(np.arange(OFF[nm] + c * 128, OFF[nm] + (c + 1) * 128))
    for nm in ("fx_q", "fx_k", "fx_v"):
        cols.append(np.arange(OFF[nm] + 2 * c * 128, OFF[nm] + (2 * c + 2) * 128))
    cols.append(np.arange(OFF["fx_f"] + 2 * c, OFF["fx_f"] + 2 * c + 2))
    cols.append(np.arange(OFF["hg_i"] + c * 128, OFF["hg_i"] + (c + 1) * 128))
    return np.concatenate(cols)


def _pvec_b(l, c, p):
    pv = np.zeros((128, 16), np.float32)
    sl = slice(c * 128, (c + 1) * 128)
    pv[:, 0:4] = p["lru_conv_w"][l][:, sl].T
    pv[:, 4] = p["lru_conv_b"][l][sl]
    pv[:, 5] = p["lru_gate_a_b"][l][sl]
    pv[:, 6] = p["lru_gate_x_b"][l][sl]
    pv[:, 7] = p["lru_lambda"][l][sl]
    pv[:, 8] = p["hgrn_lb_logits"][0][sl]
    pv[:, 9] = p["hgrn_lb_logits"][1][sl]
    g = p["mix_norm_w"][l]
    pv[:, 10] = g[sl]
    pv[:, 11] = g[1024 + c * 128:1024 + (c + 1) * 128]
    pv[:, 12] = g[2048 + 2 * c * 128:2048 + (2 * c + 1) * 128]
    pv[:, 13] = g[2048 + (2 * c + 1) * 128:2048 + (2 * c + 2) * 128]
    pv[:, 14] = p["fox_f_bias"][l][2 * c]
    pv[:, 15] = p["fox_f_bias"][l][2 * c + 1]
    return pv


def _with_halo(fullT, c):
    t0 = c * TOK
    out = np.zeros((fullT.shape[0], TOK + 2), fullT.dtype)
    out[:, 2:] = fullT[:, t0:t0 + TOK]
    if c % (NCORES // BATCH) != 0:
        out[:, 0:2] = fullT[:, t0 - 2:t0]
    return out


def kernel(**inp):
    p = {k: np.asarray(v) for k, v in inp.items()}
    x = p["x"].astype(np.float32, copy=False)
    cores = list(range(NCORES))
    xT_full = np.ascontiguousarray(x.reshape(BATCH * SEQ, D_MODEL).T)
    pack = lambda w: np.ascontiguousarray(w.reshape(D_MODEL // 128, 128).T)

    nca = build_phase_a()
    ims = [{"xT": np.ascontiguousarray(xT_full[:, c * TOK:(c + 1) * TOK]), "pvec": pack(p["ln1_w"][0])} for c in cores]
    res = run_bass_kernel_spmd(nca, ims, core_ids=cores)
    hnT_full = np.concatenate([res.results[c]["hno"] for c in cores], axis=1)
    cmat = make_cmat()
    ncc = build_phase_c()
    out = None
    for l in range(2):
        ncb = build_phase_b(l)
        ims = []
        for c in cores:
            ims.append({"hnT": hnT_full, "w_in": np.ascontiguousarray(p["w_in"][l][:, _w_in_cols(c)]),
                        "pvec": _pvec_b(l, c, p),
                        "gw": np.ascontiguousarray(np.concatenate([p["lru_gate_a_w"][l][c], p["lru_gate_x_w"][l][c]], axis=1)),
                        "cmat": cmat})
        res = run_bass_kernel_spmd(ncb, ims, core_ids=cores)
        mixT_full = np.zeros((D_MODEL, BATCH * SEQ), ml_dtypes.bfloat16)
        for c in cores:
            m = res.results[c]["mixo"]
            mixT_full[c * 128:(c + 1) * 128] = m[0:128]
            mixT_full[1024 + c * 128:1024 + (c + 1) * 128] = m[128:256]
            mixT_full[2048 + 2 * c * 128:2048 + (2 * c + 2) * 128] = m[256:512]
        nw = p["ln1_w"][1] if l == 0 else p["final_norm_w"]
        pvc = pack_pvec_c(p["ln2_w"][l], nw, p["ffn_conv_w"][l], p["ffn_conv_b"][l], D_MODEL, D_FF)
        ims = []
        for c in cores:
            ims.append({"xT": _with_halo(xT_full, c), "mixT": _with_halo(mixT_full, c), "w_out": p["w_out"][l],
                        "w_up": p["ffn_w_up"][l], "w_down": p["ffn_w_down"][l], "pvec": pvc})
        res = run_bass_kernel_spmd(ncc, ims, core_ids=cores)
        xT_full = np.concatenate([res.results[c]["xo"] for c in cores], axis=1)
        hnT_full = np.concatenate([res.results[c]["hno"] for c in cores], axis=1)
    out = np.ascontiguousarray(hnT_full.T).reshape(BATCH, SEQ, D_MODEL).astype(np.float32)
    return out
```
